# Optimizing a Trainium2 kernel written in Bass

```python
import math
import jax, jax.numpy as jnp
from jax import lax
import numpy as np

D_MODEL = 1024
BATCH = 1
SEQ = 16384
DEPTH = 1

HEAD_DIM = 64
FOX_HEADS = 8
FOX_WIDTH = FOX_HEADS * HEAD_DIM
DIFF_HEADS = 4
DIFF_QK_DIM = 64
DIFF_V_DIM = 2 * DIFF_QK_DIM
DIFF_WIDTH = DIFF_HEADS * DIFF_V_DIM
MIX_WIDTH = FOX_WIDTH + DIFF_WIDTH
D_FF = ((8 * D_MODEL + 767) // 768) * 256
BLOCK_Q = 128
EPS = 1e-6
NEG_INF = -1e30

SPLIT_SIZES = (
    FOX_WIDTH,
    FOX_WIDTH,
    FOX_WIDTH,
    FOX_HEADS,
    2 * DIFF_HEADS * DIFF_QK_DIM,
    2 * DIFF_HEADS * DIFF_QK_DIM,
    DIFF_WIDTH,
)
IN_WIDTH = sum(SPLIT_SIZES)

kernel_name = "hybrid_fox_diffattn_parallel_heads"


def rms_norm(x, g):
    xf = x.astype(jnp.float32)
    y = xf * lax.rsqrt(jnp.mean(xf * xf, axis=-1, keepdims=True) + EPS)
    return (y * g.astype(jnp.float32)).astype(x.dtype)


def alibi_slopes(n):
    return 2.0 ** (-8.0 * jnp.arange(1, n + 1, dtype=jnp.float32) / n)


def fox_attention(q, k, v, log_f):
    b, h, s_len, d = q.shape
    scale = 1.0 / math.sqrt(d)
    c = jnp.cumsum(log_f, axis=-1)
    pos = jnp.arange(s_len)

    def one_block(i):
        t0 = i * BLOCK_Q
        qb = lax.dynamic_slice_in_dim(q, t0, BLOCK_Q, axis=2)
        cb = lax.dynamic_slice_in_dim(c, t0, BLOCK_Q, axis=2)
        tq = t0 + jnp.arange(BLOCK_Q)
        sc = jnp.einsum('bhqd,bhkd->bhqk', qb, k).astype(jnp.float32) * scale
        sc = sc + (cb[..., :, None] - c[..., None, :])
        sc = jnp.where(pos[None, :] <= tq[:, None], sc, NEG_INF)
        p = jax.nn.softmax(sc, axis=-1)
        return jnp.einsum('bhqk,bhkd->bhqd', p.astype(v.dtype), v)

    out = lax.map(one_block, jnp.arange(s_len // BLOCK_Q))
    return jnp.transpose(out, (1, 2, 0, 3, 4)).reshape(b, h, s_len, d)


def diff_attention(q, k, v, lam, slopes):
    b, h, _, s_len, d = q.shape
    dv = v.shape[-1]
    scale = 1.0 / math.sqrt(d)
    pos = jnp.arange(s_len)

    def one_block(i):
        t0 = i * BLOCK_Q
        qb = lax.dynamic_slice_in_dim(q, t0, BLOCK_Q, axis=3)
        tq = t0 + jnp.arange(BLOCK_Q)
        sc = jnp.einsum('bhcqd,bhckd->bhcqk', qb, k).astype(jnp.float32) * scale
        dist = (tq[:, None] - pos[None, :]).astype(jnp.float32)
        sc = sc - slopes[:, None, None, None] * dist
        sc = jnp.where(pos[None, :] <= tq[:, None], sc, NEG_INF)
        p = jax.nn.softmax(sc, axis=-1)
        a = p[:, :, 0] - lam * p[:, :, 1]
        return jnp.einsum('bhqk,bhkd->bhqd', a.astype(v.dtype), v)

    out = lax.map(one_block, jnp.arange(s_len // BLOCK_Q))
    return jnp.transpose(out, (1, 2, 0, 3, 4)).reshape(b, h, s_len, dv)


def setup_inputs(seed: int = 0) -> dict:
    key = jax.random.key(seed)
    ks = jax.random.split(key, 16)
    f32 = jnp.float32
    nrm = lambda k, shape, s: jax.random.normal(k, shape, f32) * s
    x = jax.random.normal(ks[0], (BATCH, SEQ, D_MODEL), f32)
    mix_norm_g = 1.0 + nrm(ks[1], (DEPTH, D_MODEL), 0.02)
    w_in = nrm(ks[2], (DEPTH, D_MODEL, IN_WIDTH), D_MODEL ** -0.5)
    b_forget = 3.0 + nrm(ks[3], (DEPTH, FOX_HEADS), 0.5)
    lambda_q1 = nrm(ks[4], (DEPTH, DIFF_QK_DIM), 0.1)
    lambda_k1 = nrm(ks[5], (DEPTH, DIFF_QK_DIM), 0.1)
    lambda_q2 = nrm(ks[6], (DEPTH, DIFF_QK_DIM), 0.1)
    lambda_k2 = nrm(ks[7], (DEPTH, DIFF_QK_DIM), 0.1)
    diff_norm_g = 1.0 + nrm(ks[8], (DEPTH, DIFF_V_DIM), 0.02)
    w_out = nrm(ks[9], (DEPTH, MIX_WIDTH, D_MODEL), MIX_WIDTH ** -0.5)
    ffn_norm_g = 1.0 + nrm(ks[10], (DEPTH, D_MODEL), 0.02)
    w_gate = nrm(ks[11], (DEPTH, D_MODEL, D_FF), D_MODEL ** -0.5)
    w_up = nrm(ks[12], (DEPTH, D_MODEL, D_FF), D_MODEL ** -0.5)
    w_down = nrm(ks[13], (DEPTH, D_FF, D_MODEL), D_FF ** -0.5)
    final_norm_g = 1.0 + nrm(ks[14], (D_MODEL,), 0.02)
    return {"x": x, "mix_norm_g": mix_norm_g, "w_in": w_in, "b_forget": b_forget,
            "lambda_q1": lambda_q1, "lambda_k1": lambda_k1,
            "lambda_q2": lambda_q2, "lambda_k2": lambda_k2,
            "diff_norm_g": diff_norm_g, "w_out": w_out, "ffn_norm_g": ffn_norm_g,
            "w_gate": w_gate, "w_up": w_up, "w_down": w_down,
            "final_norm_g": final_norm_g}


def reference(x, mix_norm_g, w_in, b_forget, lambda_q1, lambda_k1, lambda_q2,
              lambda_k2, diff_norm_g, w_out, ffn_norm_g, w_gate, w_up, w_down,
              final_norm_g):
    b, s_len, _ = x.shape
    slopes = alibi_slopes(DIFF_HEADS)
    offsets = list(np.cumsum(SPLIT_SIZES)[:-1])
    for l in range(DEPTH):
        h = rms_norm(x, mix_norm_g[l])
        proj = jnp.einsum('bsd,de->bse', h, w_in[l])
        fq, fk, fv, flog, dq, dk, dv = jnp.split(proj, offsets, axis=-1)

        to_heads = lambda t: jnp.transpose(t.reshape(b, s_len, FOX_HEADS, HEAD_DIM), (0, 2, 1, 3))
        log_f = jax.nn.log_sigmoid(flog.astype(jnp.float32) + b_forget[l].astype(jnp.float32))
        log_f = jnp.transpose(log_f, (0, 2, 1))
        fox_out = fox_attention(to_heads(fq), to_heads(fk), to_heads(fv), log_f)
        fox_out = jnp.transpose(fox_out, (0, 2, 1, 3)).reshape(b, s_len, FOX_WIDTH)

        qk_heads = lambda t: jnp.transpose(t.reshape(b, s_len, DIFF_HEADS, 2, DIFF_QK_DIM), (0, 2, 3, 1, 4))
        dv_h = jnp.transpose(dv.reshape(b, s_len, DIFF_HEADS, DIFF_V_DIM), (0, 2, 1, 3))
        lam_init = 0.8 - 0.6 * math.exp(-0.3 * l)
        lam = (jnp.exp(jnp.sum(lambda_q1[l].astype(jnp.float32) * lambda_k1[l].astype(jnp.float32)))
               - jnp.exp(jnp.sum(lambda_q2[l].astype(jnp.float32) * lambda_k2[l].astype(jnp.float32)))
               + lam_init)
        diff_out = diff_attention(qk_heads(dq), qk_heads(dk), dv_h, lam, slopes)
        diff_out = rms_norm(diff_out, diff_norm_g[l]) * (1.0 - lam_init)
        diff_out = jnp.transpose(diff_out, (0, 2, 1, 3)).reshape(b, s_len, DIFF_WIDTH)

        mixed = jnp.concatenate([fox_out, diff_out], axis=-1)
        x = x + jnp.einsum('bse,ed->bsd', mixed, w_out[l])

        h = rms_norm(x, ffn_norm_g[l])
        g = jnp.einsum('bsd,df->bsf', h, w_gate[l])
        u = jnp.einsum('bsd,df->bsf', h, w_up[l])
        x = x + jnp.einsum('bsf,fd->bsd', jax.nn.silu(g) * u, w_down[l])
    return rms_norm(x, final_norm_g)
```

```python
from contextlib import ExitStack
import math
import numpy as np
import concourse.bass as bass
import concourse.mybir as mybir
from concourse.bass_utils import run_bass_kernel_spmd

F32 = mybir.dt.float32
BF = mybir.dt.bfloat16
I32 = mybir.dt.int32
ALU = mybir.AluOpType
AF = mybir.ActivationFunctionType

D = 1024
DFF = 2816
NFC = 22
EPS = 1e-6
KA = 69
RAWF = 47616

C_MIXG, C_BF, C_NEGF, C_A0, C_A1C, C_DFLAG, C_KD, C_KF, C_GVEC = 0, 8, 10, 11, 12, 13, 14, 15, 16
C_LAMP = 17
C_ID = C_LAMP + 256
C_TRI = C_ID + 128
C_MASK = C_TRI + 128
C_ALIBI = C_MASK + 128


class _Rec:
    def __init__(self):
        self.call = None

    def __getattr__(self, name):
        def f(*a, **k):
            self.call = (name, a, k)
            return self
        return f


def _freeze(fn):
    if fn is None:
        return None
    r = _Rec()
    fn(r)
    assert r.call is not None
    return r.call


class Em:
    def __init__(self):
        self.q = {k: [] for k in ("pe", "act", "dve", "pool", "sp")}
        self.cnt = {"pe": 0, "act": 0, "dve": 0, "pool": 0}
        self.dmacnt = {}
        self.waited = {k: {} for k in self.q}

    def _waits(self, eng, deps):
        out = []
        for d in deps:
            if d is None:
                continue
            key, val = d
            if self.waited[eng].get(key, 0) >= val:
                continue
            self.waited[eng][key] = val
            out.append((key, val))
        return out

    def op(self, eng, fn, deps=(), inc=True):
        w = self._waits(eng, deps)
        tok = None
        incd = None
        if inc:
            self.cnt[eng] += 1
            tok = (eng, self.cnt[eng])
            incd = (eng, 1)
        self.q[eng].append((w, _freeze(fn), incd))
        return tok

    def dma(self, eng, fn, sem, deps=()):
        w = self._waits(eng, deps)
        self.dmacnt[sem] = self.dmacnt.get(sem, 0) + 16
        self.q[eng].append((w, _freeze(fn), (sem, 16)))
        return (sem, self.dmacnt[sem])

    def cc(self, fn, sem, deps=()):
        w = self._waits("pool", deps)
        self.dmacnt[sem] = self.dmacnt.get(sem, 0) + 1
        self.q["pool"].append((w, _freeze(fn), (sem, 1)))
        return (sem, self.dmacnt[sem])


def build(S, debug=False):
    NCH = S // 512
    NT = S // 128
    TPC = S // 8
    PT = min(1024, TPC)
    NPASS = TPC // PT
    TT = PT // 128
    NTC = PT // 512
    KS = C_ALIBI + NT

    nc = bass.Bass("TRN2", target_bir_lowering=False)
    dr = lambda n, sh, dt=F32: nc.dram_tensor(n, sh, dt, kind="ExternalInput").ap()
    xT = dr("xT", [D, S])
    xtok = dr("xtok", [TPC, D])
    wq_d = dr("wq", [2, D, KA])
    wk_d = dr("wk", [2, D, KA])
    wv_d = dr("wv", [D, 130])
    sm_d = dr("smalls", [128, KS])
    gidx_d = dr("gidx", [8 * NPASS, 128, 1], I32)
    wo_d = dr("w_out", [D, D])
    wg_d = dr("w_gate", [D, DFF])
    wu_d = dr("w_up", [D, DFF])
    wd_d = dr("w_down", [DFF, D])
    gb_d = dr("gbc", [128, 2, D])
    y_d = nc.dram_tensor("y", [TPC, D], F32, kind="ExternalOutput").ap()
    agin = nc.dram_tensor("agin", [8 * NPASS * 128, PT], BF)
    agout = nc.dram_tensor("agout", [8 * 8 * NPASS * 128, PT], BF)

    E = Em()
    st = ExitStack()
    with st:
        raw = st.enter_context(nc.sbuf_tensor("raw", [128, RAWF], F32))
        ps = st.enter_context(nc.psum_tensor("ps", [128, 4096], F32))
        idx_t = [st.enter_context(nc.sbuf_tensor(f"gi{k}", [128, 1], I32)) for k in range(8 * NPASS)]
        semnames = ["pe", "act", "dve", "pool", "d_x", "d_w", "d_mx", "d_c", "d_wo", "d_wg0", "d_wg1", "d_wd",
                    "d_xt", "d_g", "d_y", "cc"]
        sems = {n: st.enter_context(nc.semaphore("s_" + n)) for n in semnames}
        block = st.enter_context(nc.Block())

        class Arena:
            def __init__(self, base=0):
                self.off = base

            def f32(self, n):
                o = self.off
                self.off += n
                assert self.off <= RAWF, self.off
                return raw[:, o:o + n]

            def bf(self, n):
                n2 = (n + 1) // 2
                return self.f32(n2).bitcast(BF)[:, 0:n]

        ar = Arena()
        SM = ar.f32(KS)
        ident_bf = ar.bf(128)
        maskb = ar.bf(128)
        ones_bf = ar.bf(128)
        ones_f = ar.f32(128)
        b32 = ar.f32(2)
        a1v = ar.f32(1)
        s12 = ar.f32(2)
        e12 = ar.f32(2)
        lt = ar.f32(2)
        epsv = ar.f32(4)
        gbc = ar.f32(2 * D).rearrange("p (a d) -> p a d", a=2)
        const_end = ar.off
        ident_f = SM[:, C_ID:C_ID + 128]
        tri_f = SM[:, C_TRI:C_TRI + 128]
        col = lambda c: SM[:, c:c + 1]

        KT = [ar.bf(S), ar.bf(S)]
        Vsb = ar.bf(NT * 128).rearrange("p (t d) -> p t d", d=128)
        xf = ar.f32(4096).rearrange("p (c t) -> p c t", c=8)
        xb = ar.bf(4096).rearrange("p (c t) -> p c t", c=8)
        xsq = ar.bf(4096).rearrange("p (c t) -> p c t", c=8)
        Wq = ar.bf(2 * 8 * KA).rearrange("p (m c k) -> p m c k", m=2, c=8)
        Wk = ar.bf(2 * 8 * KA).rearrange("p (m c k) -> p m c k", m=2, c=8)
        Wv = ar.bf(8 * 130).rearrange("p (c k) -> p c k", c=8)
        QT = [[ar.bf(512) for m in range(2)] for b in range(2)]
        rp_bc = ar.f32(512)
        rp_col = ar.f32(4)
        junk = ar.f32(128)
        zt = ar.f32(8)
        et = ar.f32(8)
        spt = ar.f32(8)
        lf = ar.f32(8)
        uu = ar.f32(8)
        r1 = ar.f32(8)
        r2 = ar.f32(8)
        carry = ar.f32((NT + 1) * 2).rearrange("p (t m) -> p t m", m=2)
        CK = ar.bf(8 * KA + 8).rearrange("p (j k) -> p j k", k=KA + 1)
        CQ = ar.bf(8 * KA + 8).rearrange("p (j k) -> p j k", k=KA + 1)
        Pb = [ar.bf(1024).rearrange("p (m n) -> p m n", m=2) for _ in range(3)]
        mx = [ar.bf(512) for _ in range(2)]
        scr0 = ar.off
        rl = [ar.f32(512) for _ in range(2)]
        oo = [ar.f32(512) for _ in range(2)]
        comb = ar.f32(512)
        sq = ar.f32(512)
        rs = ar.f32(512)
        ab_end = ar.off
        ars = Arena(scr0)
        stq = ars.f32(2 * 8 * KA).rearrange("p (m c k) -> p m c k", m=2, c=8)
        stk = ars.f32(2 * 8 * KA).rearrange("p (m c k) -> p m c k", m=2, c=8)
        stv = ars.f32(8 * 130).rearrange("p (c k) -> p c k", c=8)
        assert ars.off <= ab_end

        bank = lambda b: ps[:, 512 * b:512 * (b + 1)]

        t_sm = E.dma("sp", lambda e: e.dma_start(out=SM, in_=sm_d), "d_c")
        t_gb = E.dma("sp", lambda e: e.dma_start(out=gbc, in_=gb_d), "d_c")
        t_wq = E.dma("sp", lambda e: e.dma_start(out=stq, in_=wq_d.rearrange("m (c p) k -> p m c k", p=128)), "d_w")
        t_wk = E.dma("sp", lambda e: e.dma_start(out=stk, in_=wk_d.rearrange("m (c p) k -> p m c k", p=128)), "d_w")
        t_wv = E.dma("sp", lambda e: e.dma_start(out=stv, in_=wv_d.rearrange("(c p) k -> p c k", p=128)), "d_w")
        for k in range(8 * NPASS):
            t_gi = E.dma("pool", lambda e, k=k: e.dma_start(out=idx_t[k][:], in_=gidx_d[k]), "d_g")
        t_c = (t_gb[0], t_gb[1])
        t_wq = t_wk = t_wv = (t_wv[0], t_wv[1])

        def load_x(i, deps):
            return E.dma("sp", lambda e: e.dma_start(
                out=xf, in_=xT[:, i * 512:(i + 1) * 512].rearrange("(c p) t -> p c t", p=128)), "d_x", deps)

        t_x = load_x(0, [])

        dv = lambda fn, deps=(): E.op("dve", fn, deps)
        ac = lambda fn, deps=(): E.op("act", fn, deps)
        po = lambda fn, deps=(): E.op("pool", fn, deps)

        t = dv(lambda e: e.tensor_copy(out=ident_bf, in_=ident_f), [t_c])
        t = dv(lambda e: e.tensor_copy(out=maskb, in_=SM[:, C_MASK:C_MASK + 128]))
        t = dv(lambda e: e.memset(ones_bf, 1.0))
        t = dv(lambda e: e.memset(ones_f, 1.0))
        t = dv(lambda e: e.memset(epsv[:, 0:1], 1024.0 * EPS))
        t = dv(lambda e: e.memset(epsv[:, 1:2], 128.0 * EPS))
        t_eps = t
        t = dv(lambda e: e.memset(CK, 0.0))
        t = dv(lambda e: e.memset(CQ, 0.0))
        t = dv(lambda e: e.memset(carry[:, 0, :], 0.0))
        t = dv(lambda e: e.memset(CK[:, :, 67:69], 1.0), [t])
        t = dv(lambda e: e.memset(CQ[:, :, 64:67], 1.0), [t])
        t = dv(lambda e: e.tensor_scalar(out=b32, in0=SM[:, C_BF:C_BF + 2], scalar1=1.0 / 32, scalar2=None, op0=ALU.mult))
        lamp = SM[:, C_LAMP:C_LAMP + 256].rearrange("p (a k) -> p a k", a=4)
        t = dv(lambda e: e.scalar_tensor_tensor(out=junk[:, 0:64], in0=lamp[:, 0, :], scalar=1.0, in1=lamp[:, 1, :],
                                                op0=ALU.mult, op1=ALU.mult, accum_out=s12[:, 0:1]))
        t = dv(lambda e: e.scalar_tensor_tensor(out=junk[:, 64:128], in0=lamp[:, 2, :], scalar=1.0, in1=lamp[:, 3, :],
                                                op0=ALU.mult, op1=ALU.mult, accum_out=s12[:, 1:2]))
        ta = ac(lambda e: e.activation(out=e12, in_=s12, func=AF.Exp), [t])
        t = dv(lambda e: e.tensor_tensor(out=lt[:, 0:1], in0=e12[:, 1:2], in1=e12[:, 0:1], op=ALU.subtract), [ta])
        t = dv(lambda e: e.tensor_scalar(out=lt[:, 1:2], in0=lt[:, 0:1], scalar1=-0.2, scalar2=None, op0=ALU.add), [t])
        t = dv(lambda e: e.tensor_scalar(out=lt[:, 1:2], in0=lt[:, 1:2], scalar1=col(C_DFLAG), scalar2=None, op0=ALU.mult), [t])
        t = dv(lambda e: e.tensor_tensor(out=a1v, in0=lt[:, 1:2], in1=col(C_A1C), op=ALU.add), [t])
        for m in range(2):
            for c in range(8):
                t = dv(lambda e, m=m, c=c: e.tensor_scalar(out=Wq[:, m, c, :], in0=stq[:, m, c, :],
                                                           scalar1=col(C_MIXG + c), scalar2=None, op0=ALU.mult), [t_wq])
                t = dv(lambda e, m=m, c=c: e.tensor_scalar(out=Wk[:, m, c, :], in0=stk[:, m, c, :],
                                                           scalar1=col(C_MIXG + c), scalar2=None, op0=ALU.mult), [t_wk])
        for c in range(8):
            t = dv(lambda e, c=c: e.tensor_scalar(out=Wv[:, c, :], in0=stv[:, c, :],
                                                  scalar1=col(C_MIXG + c), scalar2=None, op0=ALU.mult), [t_wv])
        t_winit = t

        def prefetch(i, t_x, deps_war):
            tc_ = po(lambda e: e.tensor_copy(out=xb, in_=xf), [t_x] + deps_war)
            ts_ = dv(lambda e: e.tensor_tensor(out=xsq, in0=xf, in1=xf, op=ALU.mult), [t_x] + deps_war)
            return tc_, ts_

        t_cast, t_sq = prefetch(0, t_x, [])

        state = {"Sfree": None}
        tokPV_hist = {}
        U = 0
        t_cmb_prev = None
        t_n2r_prev = None
        pend_n2 = None
        t_mxd = [None, None]
        t_mxd_all = []
        t_qfree = [None, None]
        t_sq2 = None

        def emit_ssq2(i_blk, t_sqtok, t_cmbtok):
            tn2 = E.op("pe", lambda e: e.matmul(bank(4), lhsT=ones_f, rhs=sq, start=True, stop=True),
                       [t_sqtok, t_cmbtok])
            t8 = ac(lambda e: e.activation(out=rs, in_=bank(4), func=AF.Ln, bias=epsv[:, 1:2]), [tn2])
            t8b = ac(lambda e: e.activation(out=rs, in_=rs, func=AF.Exp, scale=-0.5), [t8])
            t9 = dv(lambda e: e.tensor_scalar(out=rs, in0=rs, scalar1=col(C_KD), scalar2=None, op0=ALU.mult), [t8b])
            t9 = dv(lambda e: e.tensor_scalar(out=rs, in0=rs, scalar1=col(C_KF), scalar2=None, op0=ALU.add), [t9])
            mb = mx[i_blk % 2]
            t10 = dv(lambda e: e.scalar_tensor_tensor(out=mb, in0=comb, scalar=col(C_GVEC), in1=rs,
                                                      op0=ALU.mult, op1=ALU.mult), [t9, t_mxd[i_blk % 2]])
            tok0 = i_blk * 512
            b = tok0 // TPC
            tl = tok0 % TPC
            qq = tl // PT
            cc_ = tl % PT
            dst = agin.ap().rearrange("(b q f) n -> f b q n", b=8, q=NPASS)[:, b, qq, cc_:cc_ + 512]
            td = E.dma("sp", lambda e: e.dma_start(out=dst, in_=mb), "d_mx", [t10])
            t_mxd[i_blk % 2] = td
            t_mxd_all.append(td)
            return t8

        for i in range(NCH):
            t0 = i * 512
            qb = i % 2
            dep0 = [t_cast, t_sq, t_winit]
            for c in range(8):
                E.op("pe", lambda e, c=c: e.matmul(bank(0), lhsT=ones_bf, rhs=xsq[:, c, :], start=(c == 0), stop=(c == 7)),
                     dep0 if c == 0 else (), inc=False)
            vps = bank(1).rearrange("p (j d) -> p j d", j=4)
            fps = bank(2)[:, 0:8].rearrange("p (j m) -> p j m", j=4)
            for j in range(4):
                for c in range(8):
                    E.op("pe", lambda e, c=c, j=j: e.matmul(vps[:, j, :], lhsT=xb[:, c, j * 128:(j + 1) * 128],
                                                            rhs=Wv[:, c, 0:128], start=(c == 0), stop=(c == 7)), inc=False)
            tpe1 = None
            for j in range(4):
                for c in range(8):
                    last = (j == 3 and c == 7)
                    tk_ = E.op("pe", lambda e, c=c, j=j: e.matmul(fps[:, j, :], lhsT=xb[:, c, j * 128:(j + 1) * 128],
                                                                  rhs=Wv[:, c, 128:130], start=(c == 0), stop=(c == 7)),
                               inc=last)
                    if last:
                        tpe1 = tk_
            def kq_main(bk, W, m, deps=()):
                for c in range(8):
                    E.op("pe", lambda e, c=c: e.matmul(bank(bk)[0:KA, :], lhsT=W[:, m, c, :], rhs=xb[:, c, :],
                                                       start=(c == 0), stop=False), deps if c == 0 else (), inc=False)
            kq_main(3, Wk, 0)
            t = ac(lambda e: e.activation(out=rp_bc, in_=bank(0), func=AF.Ln, bias=epsv[:, 0:1]), [tpe1, t_eps])
            t = ac(lambda e: e.activation(out=rp_bc, in_=rp_bc, func=AF.Exp, scale=-0.5), [t])
            for j in range(4):
                t = dv(lambda e, j=j: e.scalar_tensor_tensor(out=junk, in0=rp_bc[:, j * 128:(j + 1) * 128], scalar=1.0,
                                                             in1=ident_f, op0=ALU.mult, op1=ALU.mult,
                                                             accum_out=rp_col[:, j:j + 1]), [t])
            t_rstd = t
            kq_main(0, Wk, 1, [t_rstd])
            for j in range(4):
                t = dv(lambda e, j=j: e.tensor_scalar(out=Vsb[:, 4 * i + j, :], in0=vps[:, j, :],
                                                      scalar1=rp_col[:, j:j + 1], scalar2=32.0, op0=ALU.mult, op1=ALU.mult),
                       [t_rstd])
            ztv = zt.rearrange("p (j m) -> p j m", j=4)
            for j in range(4):
                t = dv(lambda e, j=j: e.scalar_tensor_tensor(out=ztv[:, j, :], in0=fps[:, j, :], scalar=rp_col[:, j:j + 1],
                                                             in1=b32, op0=ALU.mult, op1=ALU.add), [t_rstd])
            t_vev = t
            kq_main(1, Wq, 0, [t_vev])
            ta = ac(lambda e: e.activation(out=et, in_=zt, func=AF.Exp, scale=-32.0), [t_vev])
            ta = ac(lambda e: e.activation(out=spt, in_=et, func=AF.Ln, bias=1.0), [ta])
            t_lf = dv(lambda e: e.tensor_scalar(out=lf, in0=spt, scalar1=col(C_NEGF), scalar2=None, op0=ALU.mult), [ta])
            E.op("pe", lambda e: e.matmul(bank(2)[:, 8:16], lhsT=tri_f, rhs=lf, start=True, stop=True), [t_lf], inc=False)
            tcum = E.op("pe", lambda e: e.matmul(bank(2)[:, 16:24], lhsT=ones_f, rhs=lf, start=True, stop=True))
            cumv = bank(2)[:, 8:16].rearrange("p (j m) -> p j m", j=4)
            totv = bank(2)[:, 16:24].rearrange("p (j m) -> p j m", j=4)
            uv = uu.rearrange("p (j m) -> p j m", j=4)
            t = tcum
            for j in range(4):
                g = 4 * i + j
                t = dv(lambda e, j=j, g=g: e.scalar_tensor_tensor(out=uv[:, j, :], in0=cumv[:, j, :],
                                                                  scalar=col(C_ALIBI + g), in1=carry[:, g, :],
                                                                  op0=ALU.add, op1=ALU.add), [t])
                t = dv(lambda e, j=j, g=g: e.tensor_tensor(out=carry[:, g + 1, :], in0=carry[:, g, :], in1=totv[:, j, :],
                                                           op=ALU.add), [t])
            t = dv(lambda e: e.tensor_scalar(out=CK[:, :, 64], in0=uu, scalar1=-1.0, scalar2=None, op0=ALU.mult),
                   [t, t_qfree[0], t_qfree[1]])
            t = dv(lambda e: e.scalar_tensor_tensor(out=r1, in0=uu, scalar=-1.0, in1=CK[:, :, 64],
                                                    op0=ALU.mult, op1=ALU.subtract), [t])
            t = dv(lambda e: e.tensor_copy(out=CK[:, :, 65], in_=r1), [t])
            t = dv(lambda e: e.tensor_tensor(out=r2, in0=r1, in1=CK[:, :, 65], op=ALU.subtract), [t])
            t = dv(lambda e: e.tensor_copy(out=CK[:, :, 66], in_=r2), [t])
            t = dv(lambda e: e.tensor_scalar(out=CQ[:, :, 67], in0=CK[:, :, 64], scalar1=-1.0, scalar2=None, op0=ALU.mult), [t])
            t = dv(lambda e: e.tensor_scalar(out=CQ[:, :, 68], in0=CK[:, :, 65], scalar1=-1.0, scalar2=None, op0=ALU.mult), [t])
            t_caug = t
            kq_main(2, Wq, 1, [t_caug])
            tkq = {}
            for (bk, CC, m, nm) in ((3, CK, 0, "k0"), (0, CK, 1, "k1"), (1, CQ, 0, "q0"), (2, CQ, 1, "q1")):
                for j in range(4):
                    tkq[nm] = E.op("pe", lambda e, bk=bk, CC=CC, m=m, j=j: e.matmul(
                        bank(bk)[0:KA, j * 128:(j + 1) * 128], lhsT=CC[:, 2 * j + m, 0:KA], rhs=ident_bf,
                        start=False, stop=True), inc=(j == 3))
            t_peA = tkq["q1"]
            for (bk, m, nm) in ((3, 0, "k0"), (0, 1, "k1")):
                t = dv(lambda e, bk=bk, m=m: e.tensor_tensor(out=KT[m][0:64, t0:t0 + 512], in0=bank(bk)[0:64, :],
                                                             in1=rp_bc[0:64, :], op=ALU.mult), [tkq[nm]])
                t = dv(lambda e, bk=bk, m=m: e.tensor_copy(out=KT[m][64:KA, t0:t0 + 512], in_=bank(bk)[64:KA, :]), [tkq[nm]])
            for (bk, m, nm) in ((1, 0, "q0"), (2, 1, "q1")):
                t = dv(lambda e, bk=bk, m=m: e.scalar_tensor_tensor(out=QT[qb][m][0:64, :], in0=bank(bk)[0:64, :],
                                                                    scalar=128.0, in1=rp_bc[0:64, :],
                                                                    op0=ALU.mult, op1=ALU.mult), [tkq[nm], t_qfree[qb]])
                t = dv(lambda e, bk=bk, m=m: e.tensor_copy(out=QT[qb][m][64:KA, :], in_=bank(bk)[64:KA, :]), [tkq[nm]])
            t_kqev = t

            if pend_n2 is not None:
                t_n2r_prev = emit_ssq2(*pend_n2)
                pend_n2 = None

            if i + 1 < NCH:
                t_x = load_x(i + 1, [t_cast, t_sq])
                t_cast, t_sq = prefetch(i + 1, t_x, [t_peA])

            units = list(range(4 * i + 4))
            nU = len(units)
            tokS = {}
            tokP = {}

            def emit_qk(g):
                a = g - 4 * i
                lo = 128 * a if a > 0 else 0
                diag = a >= 0
                buf = (U + g) % 2
                last_tok = None
                for m in range(2):
                    bk = 2 * buf + m
                    fin = (m == 1)
                    tk_ = E.op("pe", lambda e, bk=bk, m=m, lo=lo: e.matmul(
                        bank(bk)[:, lo:512], lhsT=KT[m][0:KA, g * 128:(g + 1) * 128], rhs=QT[qb][m][0:KA, lo:512],
                        start=True, stop=(not diag)), [t_kqev] if g == 0 else (), inc=(fin and not diag))
                    if diag:
                        tk_ = E.op("pe", lambda e, bk=bk, lo=lo: e.matmul(
                            bank(bk)[:, lo:lo + 128], lhsT=ident_bf, rhs=maskb, start=False, stop=True), inc=fin)
                    last_tok = tk_
                tokS[g] = last_tok
                Ug = U + g
                pb = Ug % 3
                sview = ps[:, 1024 * buf:1024 * (buf + 1)].rearrange("p (m n) -> p m n", m=2)[:, :, lo:512]
                tokP[g] = ac(lambda e, pb=pb, lo=lo, sview=sview: e.activation(out=Pb[pb][:, :, lo:512], in_=sview, func=AF.Exp),
                             [tokS[g], tokPV_hist.get(Ug - 3)])

            def emit_pv(g):
                a = g - 4 * i
                lo = 128 * a if a > 0 else 0
                Ug = U + g
                pb = Ug % 3
                first = (g == 0)
                lastu = (g == nU - 1)
                deps = [tokP[g]]
                if first:
                    deps += [t_cmb_prev, t_n2r_prev]
                tk_ = None
                for m in range(2):
                    E.op("pe", lambda e, m=m, lo=lo, pb=pb: e.matmul(bank(4 + m)[:, lo:512], lhsT=Vsb[:, g, :],
                                                                     rhs=Pb[pb][:, m, lo:512], start=first, stop=lastu),
                         deps if m == 0 else (), inc=False)
                    tk_ = E.op("pe", lambda e, m=m, lo=lo, pb=pb: e.matmul(bank(6 + m)[:, lo:512], lhsT=ones_bf,
                                                                           rhs=Pb[pb][:, m, lo:512], start=first, stop=lastu),
                               inc=(m == 1))
                tokPV_hist[Ug] = tk_
                return tk_

            tlast = None
            for idx, g in enumerate(units):
                emit_qk(g)
                if idx >= 1:
                    tlast = emit_pv(units[idx - 1])
            tlast = emit_pv(units[-1])
            t_acc = tlast
            t_qfree[qb] = t_acc
            U += nU
            for m in range(2):
                t = dv(lambda e, m=m: e.reciprocal(out=rl[m], in_=bank(6 + m)), [t_acc])
                t = dv(lambda e, m=m: e.tensor_tensor(out=oo[m], in0=bank(4 + m), in1=rl[m], op=ALU.mult), [t])
            t_cmb_prev = t
            t = dv(lambda e: e.tensor_scalar(out=oo[1], in0=oo[1], scalar1=a1v, scalar2=None, op0=ALU.mult), [t])
            t = dv(lambda e: e.scalar_tensor_tensor(out=comb, in0=oo[0], scalar=col(C_A0), in1=oo[1],
                                                    op0=ALU.mult, op1=ALU.add), [t])
            t_sq2 = dv(lambda e: e.tensor_tensor(out=sq, in0=comb, in1=comb, op=ALU.mult), [t])
            pend_n2 = (i, t_sq2, t_cmb_prev)

        t_n2r_prev = emit_ssq2(*pend_n2)

        if debug:
            dbg_items = [("rp_bc", rp_bc, F32), ("rp_col", rp_col, F32), ("Vsb", Vsb.rearrange("p t d -> p (t d)"), BF),
                         ("KT0", KT[0], BF), ("KT1", KT[1], BF), ("QT0", QT[(NCH - 1) % 2][0], BF), ("QT1", QT[(NCH - 1) % 2][1], BF),
                         ("uu", uu, F32), ("carry", carry.rearrange("p t m -> p (t m)"), F32), ("lf", lf, F32), ("zt", zt, F32),
                         ("oo0", oo[0], F32), ("oo1", oo[1], F32), ("rl0", rl[0], F32), ("rl1", rl[1], F32), ("comb", comb, F32),
                         ("rs", rs, F32), ("mx", mx[(NCH - 1) % 2], BF), ("Pb0", Pb[0].rearrange("p m n -> p (m n)"), BF),
                         ("Wq", Wq.rearrange("p m c k -> p (m c k)"), BF), ("a1v", a1v, F32), ("xb", xb.rearrange("p c t -> p (c t)"), BF), ("Wv", Wv.rearrange("p c k -> p (c k)"), BF), ("Wk", Wk.rearrange("p m c k -> p (m c k)"), BF)]
            done = [("d_mx", E.dmacnt["d_mx"]), ("dve", E.cnt["dve"]), ("act", E.cnt["act"]), ("pe", E.cnt["pe"])]
            for (nm, apx, dt_) in dbg_items:
                dd = nc.dram_tensor("dbg_" + nm, [128, apx.shape[1]], dt_, kind="ExternalOutput").ap()
                E.dma("sp", lambda e, dd=dd, apx=apx: e.dma_start(out=dd, in_=apx), "d_y", done)
            E.q["sp"].append(([("d_y", E.dmacnt["d_y"])], None, None))
        if not debug:
            arc = Arena(const_end)
            xtk = arc.f32(TT * D).rearrange("p (t d) -> p t d", t=TT)
            mT = arc.bf(8 * PT).rearrange("p (s n) -> p s n", s=8)
            hT = mT
            hbf = [arc.bf(D) for _ in range(2)]
            aT = arc.bf(NFC * PT).rearrange("p (f n) -> p f n", f=NFC)
            Wd = arc.bf(NFC * D).rearrange("p (f d) -> p f d", f=NFC)
            Wo = arc.bf(8 * 512).rearrange("p (s d) -> p s d", s=8)
            WG = [arc.bf(8 * 256).rearrange("p (c f) -> p c f", c=8) for _ in range(2)]
            WU = [arc.bf(8 * 256).rearrange("p (c f) -> p c f", c=8) for _ in range(2)]
            sg = [arc.f32(512) for _ in range(2)]
            ssq1 = arc.f32(TT)
            r1c = arc.f32(TT)
            ssq3 = arc.f32(TT)
            r3c = arc.f32(TT)

            all_done = list(t_mxd_all) + [t_n2r_prev]
            t_ag = E.cc(lambda e: e.collective_compute("AllGather", ALU.bypass, replica_groups=[list(range(8))],
                                                       ins=[agin.ap().opt()], outs=[agout.ap().opt()]), "cc",
                        [t_mxd_all[-1], t_mxd_all[-2] if len(t_mxd_all) > 1 else None, ("d_mx", E.dmacnt["d_mx"])])
            phase_b_done = [("d_mx", E.dmacnt["d_mx"]), ("dve", E.cnt["dve"]), ("act", E.cnt["act"]), ("pe", E.cnt["pe"])]
            t_g32 = dv(lambda e: e.tensor_scalar(out=gbc, in0=gbc, scalar1=32.0, scalar2=None, op0=ALU.mult), [t_c])
            t_wd = E.dma("pool", lambda e: e.dma_start(out=Wd, in_=wd_d.rearrange("(f p) d -> p f d", p=128)), "d_wd",
                         phase_b_done)

            bankfree = {b: None for b in range(8)}
            t_wo_free = None
            t_wg_free = [None, None]
            t_hbf_free = [None, None]
            t_sg_free = [None, None]
            t_prev_pass = list(phase_b_done)
            hcnt = 0
            gucnt = 0
            for q in range(NPASS):
                t_xt = E.dma("sp", lambda e, q=q: e.dma_start(
                    out=xtk, in_=xtok[q * PT:(q + 1) * PT, :].rearrange("(t p) d -> p t d", p=128)), "d_xt", t_prev_pass)
                for s in range(8):
                    k = s * NPASS + q
                    t_gth = E.dma("pool", lambda e, s=s, k=k: e.indirect_dma_start(
                        out=mT[:, s, :], out_offset=None, in_=agout.ap(),
                        in_offset=bass.IndirectOffsetOnAxis(ap=idx_t[k][:, 0:1], axis=0)), "d_g",
                        [t_ag, t_gi] + t_prev_pass)
                t_gth = ("d_g", E.dmacnt["d_g"])
                t_x1 = {}
                for dh in range(2):
                    t_wo = E.dma("pool", lambda e, dh=dh: e.dma_start(
                        out=Wo, in_=wo_d[:, dh * 512:(dh + 1) * 512].rearrange("(s p) d -> p s d", p=128)), "d_wo",
                        [t_wo_free] + t_prev_pass)
                    for tt in range(TT):
                        bk = (dh * TT + tt) % 4
                        for s in range(8):
                            tk_ = E.op("pe", lambda e, s=s, tt=tt, bk=bk: e.matmul(
                                bank(bk), lhsT=mT[:, s, tt * 128:(tt + 1) * 128], rhs=Wo[:, s, :],
                                start=(s == 0), stop=(s == 7)),
                                [t_wo, t_gth, bankfree[bk]] if s == 0 else (), inc=(s == 7))
                        t = dv(lambda e, tt=tt, dh=dh, bk=bk: e.tensor_tensor(
                            out=xtk[:, tt, dh * 512:(dh + 1) * 512], in0=bank(bk), in1=xtk[:, tt, dh * 512:(dh + 1) * 512],
                            op=ALU.add), [tk_, t_xt])
                        bankfree[bk] = t
                        t_x1[(tt, dh)] = t
                    t_wo_free = tk_
                t_hT_all = None
                for tt in range(TT):
                    hb = hbf[hcnt % 2]
                    ta = ac(lambda e, tt=tt, hb=hb: e.activation(out=hb, in_=xtk[:, tt, :], func=AF.Square,
                                                                 accum_out=ssq1[:, tt:tt + 1]),
                            [t_x1[(tt, 0)], t_x1[(tt, 1)], t_hbf_free[hcnt % 2]])
                    t = ac(lambda e, tt=tt: e.activation(out=r1c[:, tt:tt + 1], in_=ssq1[:, tt:tt + 1], func=AF.Ln,
                                                         bias=epsv[:, 0:1]), [ta])
                    t = ac(lambda e, tt=tt: e.activation(out=r1c[:, tt:tt + 1], in_=r1c[:, tt:tt + 1], func=AF.Exp, scale=-0.5), [t])
                    t = dv(lambda e, tt=tt, hb=hb: e.scalar_tensor_tensor(out=hb, in0=xtk[:, tt, :], scalar=r1c[:, tt:tt + 1],
                                                                          in1=gbc[:, 0, :], op0=ALU.mult, op1=ALU.mult),
                           [t, t_g32])
                    bk = 4 + (hcnt % 4)
                    pst = bank(bk).bitcast(BF).rearrange("p (c n) -> p c n", c=8)
                    for c in range(8):
                        tk_ = E.op("pe", lambda e, c=c, hb=hb, pst=pst: e.transpose(out=pst[:, c, :], in_=hb[:, c * 128:(c + 1) * 128],
                                                                                   identity=ident_bf),
                                   [t, bankfree[bk]] if c == 0 else (), inc=(c == 7))
                    t_hbf_free[hcnt % 2] = tk_
                    t = dv(lambda e, tt=tt, pst=pst: e.tensor_copy(out=hT[:, :, tt * 128:(tt + 1) * 128], in_=pst),
                           [tk_, t_wo_free])
                    bankfree[bk] = t
                    t_hT_all = t
                    hcnt += 1
                t_aT = None
                for fp in range(NFC // 2):
                    wb = gucnt % 2
                    t_wgl = E.dma("pool", lambda e, fp=fp, wb=wb: e.dma_start(
                        out=WG[wb], in_=wg_d[:, fp * 256:(fp + 1) * 256].rearrange("(c p) f -> p c f", p=128)), "d_wg%d" % wb,
                        [t_wg_free[wb]] + t_prev_pass)
                    t_wul = E.dma("pool", lambda e, fp=fp, wb=wb: e.dma_start(
                        out=WU[wb], in_=wu_d[:, fp * 256:(fp + 1) * 256].rearrange("(c p) f -> p c f", p=128)), "d_wg%d" % wb,
                        [t_wg_free[wb]] + t_prev_pass)
                    tk_ = None
                    for fl in range(2):
                        fc = 2 * fp + fl
                        for tc_i in range(NTC):
                            pr = (fl * NTC + tc_i) % 2
                            bg, bu = 2 * pr, 2 * pr + 1
                            for c in range(8):
                                E.op("pe", lambda e, c=c, fl=fl, tc_i=tc_i, bg=bg, wb=wb: e.matmul(
                                    bank(bg), lhsT=WG[wb][:, c, fl * 128:(fl + 1) * 128], rhs=hT[:, c, tc_i * 512:(tc_i + 1) * 512],
                                    start=(c == 0), stop=(c == 7)),
                                    [t_wgl, t_wul, t_hT_all, bankfree[bg], bankfree[bu]] if c == 0 else (), inc=False)
                            for c in range(8):
                                tk_ = E.op("pe", lambda e, c=c, fl=fl, tc_i=tc_i, bu=bu, wb=wb: e.matmul(
                                    bank(bu), lhsT=WU[wb][:, c, fl * 128:(fl + 1) * 128], rhs=hT[:, c, tc_i * 512:(tc_i + 1) * 512],
                                    start=(c == 0), stop=(c == 7)), inc=(c == 7))
                            sgb = sg[pr]
                            ta = ac(lambda e, bg=bg, sgb=sgb: e.activation(out=sgb, in_=bank(bg), func=AF.Silu),
                                    [tk_, t_sg_free[pr]])
                            t = dv(lambda e, bu=bu, sgb=sgb, fc=fc, tc_i=tc_i: e.tensor_tensor(
                                out=aT[:, fc, tc_i * 512:(tc_i + 1) * 512], in0=bank(bu), in1=sgb, op=ALU.mult), [ta, tk_])
                            t_sg_free[pr] = t
                            bankfree[bg] = t
                            bankfree[bu] = t
                            t_aT = t
                    t_wg_free[wb] = tk_
                    gucnt += 1
                t_y = None
                for tt in range(TT):
                    t_x2 = []
                    for dh in range(2):
                        bk = 4 + ((tt * 2 + dh) % 4)
                        for fc in range(NFC):
                            tk_ = E.op("pe", lambda e, fc=fc, tt=tt, dh=dh, bk=bk: e.matmul(
                                bank(bk), lhsT=aT[:, fc, tt * 128:(tt + 1) * 128], rhs=Wd[:, fc, dh * 512:(dh + 1) * 512],
                                start=(fc == 0), stop=(fc == NFC - 1)),
                                [t_aT, t_wd, bankfree[bk]] if fc == 0 else (), inc=(fc == NFC - 1))
                        t = dv(lambda e, tt=tt, dh=dh, bk=bk: e.tensor_tensor(
                            out=xtk[:, tt, dh * 512:(dh + 1) * 512], in0=bank(bk), in1=xtk[:, tt, dh * 512:(dh + 1) * 512],
                            op=ALU.add), [tk_])
                        bankfree[bk] = t
                        t_x2.append(t)
                    hb = hbf[hcnt % 2]
                    ta = ac(lambda e, tt=tt, hb=hb: e.activation(out=hb, in_=xtk[:, tt, :], func=AF.Square,
                                                                 accum_out=ssq3[:, tt:tt + 1]),
                            t_x2 + [t_hbf_free[hcnt % 2]])
                    t_hbf_free[hcnt % 2] = ta
                    hcnt += 1
                    t = ac(lambda e, tt=tt: e.activation(out=r3c[:, tt:tt + 1], in_=ssq3[:, tt:tt + 1], func=AF.Ln,
                                                         bias=epsv[:, 0:1]), [ta])
                    t = ac(lambda e, tt=tt: e.activation(out=r3c[:, tt:tt + 1], in_=r3c[:, tt:tt + 1], func=AF.Exp, scale=-0.5), [t])
                    t = dv(lambda e, tt=tt: e.scalar_tensor_tensor(out=xtk[:, tt, :], in0=xtk[:, tt, :], scalar=r3c[:, tt:tt + 1],
                                                                   in1=gbc[:, 1, :], op0=ALU.mult, op1=ALU.mult), [t])
                    r0 = q * PT + tt * 128
                    t_y = E.dma("sp", lambda e, tt=tt, r0=r0: e.dma_start(out=y_d[r0:r0 + 128, :], in_=xtk[:, tt, :]), "d_y", [t])
                t_prev_pass = [("d_y", E.dmacnt["d_y"]), ("pe", E.cnt["pe"]), ("dve", E.cnt["dve"]), ("act", E.cnt["act"])]
            E.q["sp"].append(([("d_y", E.dmacnt["d_y"])], None, None))

        def replay(name):
            def run(e):
                for (w, fn, inc) in E.q[name]:
                    for (key, val) in w:
                        e.wait_ge(sems[key], val)
                    if fn is None:
                        continue
                    ins = getattr(e, fn[0])(*fn[1], **fn[2])
                    if inc is not None:
                        ins.then_inc(sems[inc[0]], inc[1])
            return run

        block.tensor(replay("pe"))
        block.scalar(replay("act"))
        block.vector(replay("dve"))
        block.gpsimd(replay("pool"))
        block.sync(replay("sp"))
    return nc


def alibi_slopes(n):
    return (2.0 ** (-8.0 * np.arange(1, n + 1, dtype=np.float32) / n)).astype(np.float32)


def make_in_maps(S, x, mix_norm_g, w_in, b_forget, lambda_q1, lambda_k1, lambda_q2, lambda_k2, diff_norm_g, w_out,
                 ffn_norm_g, w_gate, w_up, w_down, final_norm_g):
    NT = S // 128
    TPC = S // 8
    PT = min(1024, TPC)
    NPASS = TPC // PT
    KS = C_ALIBI + NT
    f32 = np.float32
    x2 = np.ascontiguousarray(x[0], dtype=f32)
    xT = np.ascontiguousarray(x2.T)
    w = np.asarray(w_in[0], dtype=f32)
    slopes = alibi_slopes(4)
    pos = np.arange(S, dtype=f32)
    ident = np.eye(128, dtype=f32)
    kk = np.arange(128)
    tri = (kk[:, None] <= kk[None, :]).astype(f32)
    maskb = np.where(kk[:, None] <= kk[None, :], 0.0, -30000.0).astype(f32)
    lamp = np.concatenate([np.asarray(a[0], dtype=f32) for a in (lambda_q1, lambda_k1, lambda_q2, lambda_k2)])
    gbc = np.stack([np.tile(np.asarray(ffn_norm_g[0], f32)[None, :], (128, 1)),
                    np.tile(np.asarray(final_norm_g, f32)[None, :], (128, 1))], axis=1)
    gbc = np.ascontiguousarray(gbc)
    shared = dict(xT=xT, w_out=np.ascontiguousarray(w_out[0], dtype=f32), w_gate=np.ascontiguousarray(w_gate[0], dtype=f32),
                  w_up=np.ascontiguousarray(w_up[0], dtype=f32), w_down=np.ascontiguousarray(w_down[0], dtype=f32), gbc=gbc)
    maps = []
    for r in range(8):
        wq = np.zeros((2, D, KA), f32)
        wk = np.zeros((2, D, KA), f32)
        wv = np.zeros((D, 130), f32)
        sm = np.zeros((128, KS), f32)
        sm[:, C_MIXG:C_MIXG + 8] = np.asarray(mix_norm_g[0], f32).reshape(8, 128).T
        if r < 4:
            for m in range(2):
                h = 2 * r + m
                wq[m, :, :64] = w[:, 64 * h:64 * h + 64]
                wk[m, :, :64] = w[:, 512 + 64 * h:512 + 64 * h + 64]
                wv[:, 64 * m:64 * m + 64] = w[:, 1024 + 64 * h:1024 + 64 * h + 64]
                wv[:, 128 + m] = w[:, 1536 + h]
                sm[:, C_BF + m] = np.asarray(b_forget[0], f32)[h]
            sm[:, C_NEGF] = -1.0
            sm[:64, C_A0] = 1.0
            sm[64:, C_A1C] = 1.0
            sm[:, C_DFLAG] = 0.0
            sm[:, C_KD] = 0.0
            sm[:, C_KF] = 1.0
            sm[:, C_GVEC] = 1.0
        else:
            h = r - 4
            for m in range(2):
                wq[m, :, :64] = w[:, 1544 + 128 * h + 64 * m:1544 + 128 * h + 64 * m + 64]
                wk[m, :, :64] = w[:, 2056 + 128 * h + 64 * m:2056 + 128 * h + 64 * m + 64]
            wv[:, :128] = w[:, 2568 + 128 * h:2568 + 128 * h + 128]
            sm[:, C_NEGF] = 0.0
            sm[:, C_A0] = 1.0
            sm[:, C_A1C] = 0.0
            sm[:, C_DFLAG] = 1.0
            sm[:, C_KD] = np.float32(0.8 * math.sqrt(128.0))
            sm[:, C_KF] = 0.0
            sm[:, C_GVEC] = np.asarray(diff_norm_g[0], f32)
            sm[:, C_ALIBI:C_ALIBI + NT] = (-slopes[h] * pos).reshape(NT, 128).T
        sm[:, C_LAMP:C_LAMP + 256] = lamp[None, :]
        sm[:, C_ID:C_ID + 128] = ident
        sm[:, C_TRI:C_TRI + 128] = tri
        sm[:, C_MASK:C_MASK + 128] = maskb
        gidx = np.zeros((8 * NPASS, 128, 1), np.int32)
        for s in range(8):
            for q in range(NPASS):
                gidx[s * NPASS + q, :, 0] = s * (8 * NPASS * 128) + (r * NPASS + q) * 128 + np.arange(128)
        mp = dict(shared)
        mp.update(xtok=np.ascontiguousarray(x2[r * TPC:(r + 1) * TPC]), wq=wq, wk=wk, wv=wv, smalls=sm, gidx=gidx)
        maps.append(mp)
    return maps


_NC_CACHE = {}


def kernel(**inputs):
    S = inputs["x"].shape[1]
    if S not in _NC_CACHE:
        _NC_CACHE[S] = build(S)
    nc = _NC_CACHE[S]
    maps = make_in_maps(S, **inputs)
    res = run_bass_kernel_spmd(nc, maps, core_ids=list(range(8)))
    y = np.concatenate([np.asarray(res.results[r]["y"], dtype=np.float32) for r in range(8)], axis=0)
    return y[None, :, :]
```
